# Optimizing a Trainium2 kernel written in Bass

```python
import jax, jax.numpy as jnp
from jax import lax
import numpy as np

D_MODEL = 2048
BATCH = 4
SEQ = 4096
DEPTH = 1

CHUNK = 64
D_MIX = D_MODEL
SB_WIDTH = D_MIX // 2
LRU_WIDTH = D_MIX - SB_WIDTH
SB_HEAD_DIM = 128
SB_HEADS = SB_WIDTH // SB_HEAD_DIM
LRU_BLOCKS = 8
LRU_BLOCK_DIM = LRU_WIDTH // LRU_BLOCKS
CONV_WIDTH = 4
LRU_C = 8.0
Q_BLOCK = 128
EPS = 1e-6
D_IN = 4 * SB_WIDTH + 2 * LRU_WIDTH

kernel_name = "hybrid_stickbreaking_rglru_block"


def rmsnorm(x, gain):
    xf = x.astype(jnp.float32)
    y = xf * lax.rsqrt(jnp.mean(xf * xf, axis=-1, keepdims=True) + EPS)
    return (y * gain.astype(jnp.float32)).astype(x.dtype)


def stick_breaking_attention(q, k, v):
    B, H, S, Dh = q.shape
    n_blk = S // Q_BLOCK
    q_blocks = q.reshape(B, H, n_blk, Q_BLOCK, Dh).transpose(2, 0, 1, 3, 4)
    k_pos = jnp.arange(S, dtype=jnp.int32)
    scale = Dh ** -0.5

    def one_block(args):
        q_blk, blk = args
        z = jnp.einsum('bhqd,bhkd->bhqk', q_blk, k).astype(jnp.float32) * scale
        q_pos = blk * Q_BLOCK + jnp.arange(Q_BLOCK, dtype=jnp.int32)
        mask = k_pos[None, :] < q_pos[:, None]
        log_fail = jnp.where(mask, jax.nn.log_sigmoid(-z), 0.0)
        after = lax.cumsum(log_fail, axis=3, reverse=True) - log_fail
        log_w = jax.nn.log_sigmoid(z) + after
        w = jnp.where(mask, jnp.exp(log_w), 0.0)
        return jnp.einsum('bhqk,bhkd->bhqd', w.astype(v.dtype), v)

    out = lax.map(one_block, (q_blocks, jnp.arange(n_blk, dtype=jnp.int32)))
    return out.transpose(1, 2, 0, 3, 4).reshape(B, H, S, Dh)


def causal_depthwise_conv(x, w, b):
    S = x.shape[1]
    xp = jnp.pad(x, ((0, 0), (CONV_WIDTH - 1, 0), (0, 0)))
    out = xp[:, 0:S, :] * w[0]
    for i in range(1, CONV_WIDTH):
        out = out + xp[:, i:i + S, :] * w[i]
    return out + b


def rg_lru(x, w_a, b_a, w_x, b_x, lam):
    B, S, C = x.shape
    xb = x.reshape(B, S, LRU_BLOCKS, LRU_BLOCK_DIM)
    r = jax.nn.sigmoid(jnp.einsum('bsgi,gij->bsgj', xb, w_a).reshape(B, S, C) + b_a)
    i = jax.nn.sigmoid(jnp.einsum('bsgi,gij->bsgj', xb, w_x).reshape(B, S, C) + b_x)
    log_a = -LRU_C * r.astype(jnp.float32) * jax.nn.softplus(-lam.astype(jnp.float32))
    a = jnp.exp(log_a)
    u = jnp.sqrt(-jnp.expm1(2.0 * log_a)) * (i * x).astype(jnp.float32)

    def combine(left, right):
        a_l, b_l = left
        a_r, b_r = right
        return a_l * a_r, a_r * b_l + b_r

    _, h = lax.associative_scan(combine, (a, u), axis=1)
    return h.astype(x.dtype)


def setup_inputs(seed: int = 0) -> dict:
    key = jax.random.key(seed)
    ks = jax.random.split(key, 14)
    L = DEPTH
    x = jax.random.normal(ks[0], (BATCH, SEQ, D_MODEL), jnp.float32)
    norm_gain = 1.0 + 0.02 * jax.random.normal(ks[1], (L, D_MODEL), jnp.float32)
    w_in = jax.random.normal(ks[2], (L, D_MODEL, D_IN), jnp.float32) * D_MODEL ** -0.5
    q_norm_gain = 1.0 + 0.02 * jax.random.normal(ks[3], (L, SB_HEAD_DIM), jnp.float32)
    k_norm_gain = 1.0 + 0.02 * jax.random.normal(ks[4], (L, SB_HEAD_DIM), jnp.float32)
    conv_w = jax.random.normal(ks[5], (L, CONV_WIDTH, LRU_WIDTH), jnp.float32) * CONV_WIDTH ** -0.5
    conv_b = 0.01 * jax.random.normal(ks[6], (L, LRU_WIDTH), jnp.float32)
    lru_w_a = jax.random.normal(ks[7], (L, LRU_BLOCKS, LRU_BLOCK_DIM, LRU_BLOCK_DIM), jnp.float32) * LRU_BLOCK_DIM ** -0.5
    lru_b_a = 0.01 * jax.random.normal(ks[8], (L, LRU_WIDTH), jnp.float32)
    lru_w_x = jax.random.normal(ks[9], (L, LRU_BLOCKS, LRU_BLOCK_DIM, LRU_BLOCK_DIM), jnp.float32) * LRU_BLOCK_DIM ** -0.5
    lru_b_x = 0.01 * jax.random.normal(ks[10], (L, LRU_WIDTH), jnp.float32)
    u = jax.random.uniform(ks[11], (L, LRU_WIDTH), jnp.float32, minval=0.9, maxval=0.999)
    s = u ** (1.0 / LRU_C)
    lru_lambda = jnp.log(s) - jnp.log1p(-s)
    w_out = jax.random.normal(ks[12], (L, D_MIX, D_MODEL), jnp.float32) * D_MIX ** -0.5
    return {"x": x, "norm_gain": norm_gain, "w_in": w_in, "q_norm_gain": q_norm_gain,
            "k_norm_gain": k_norm_gain, "conv_w": conv_w, "conv_b": conv_b,
            "lru_w_a": lru_w_a, "lru_b_a": lru_b_a, "lru_w_x": lru_w_x, "lru_b_x": lru_b_x,
            "lru_lambda": lru_lambda, "w_out": w_out}


def reference(x, norm_gain, w_in, q_norm_gain, k_norm_gain, conv_w, conv_b,
              lru_w_a, lru_b_a, lru_w_x, lru_b_x, lru_lambda, w_out):
    B, S, _ = x.shape
    h = x
    for l in range(DEPTH):
        xn = rmsnorm(h, norm_gain[l])
        proj = jnp.einsum('bsd,de->bse', xn, w_in[l])
        q, k, v, g_sb, x_lru, g_lru = jnp.split(
            proj, np.cumsum([SB_WIDTH] * 4 + [LRU_WIDTH]).tolist(), axis=-1)

        def heads(t):
            return t.reshape(B, S, SB_HEADS, SB_HEAD_DIM)
        qh = rmsnorm(heads(q), q_norm_gain[l]).transpose(0, 2, 1, 3)
        kh = rmsnorm(heads(k), k_norm_gain[l]).transpose(0, 2, 1, 3)
        vh = heads(v).transpose(0, 2, 1, 3)
        y_sb = stick_breaking_attention(qh, kh, vh).transpose(0, 2, 1, 3).reshape(B, S, SB_WIDTH)
        y_sb = y_sb * jax.nn.silu(g_sb)

        xc = causal_depthwise_conv(x_lru, conv_w[l], conv_b[l])
        y_lru = rg_lru(xc, lru_w_a[l], lru_b_a[l], lru_w_x[l], lru_b_x[l], lru_lambda[l])
        y_lru = y_lru * jax.nn.silu(g_lru)

        y = jnp.concatenate([y_sb, y_lru], axis=-1)
        h = h + jnp.einsum('bse,ed->bsd', y, w_out[l])
    return h
```

```python
import numpy as np
import ml_dtypes
import concourse.bass as bass
import concourse.mybir as mybir
from concourse.bass_utils import run_bass_kernel_spmd

F32 = mybir.dt.float32
BF16 = mybir.dt.bfloat16
AF = mybir.ActivationFunctionType
ALU = mybir.AluOpType

D = 2048
S = 4096
B = 4
NT = 8
TS = 512
DH = 128
NH = 8
EPS = 1e-6
NPRM = 96

ENGS = ("pe", "act", "dve", "pool", "sp")


class Op:
    __slots__ = ("eng", "fn", "reads", "writes", "dma", "key", "deps", "idx",
                 "signal", "count", "waits")

    def __init__(self, eng, fn, reads, writes, dma, key):
        self.eng = eng
        self.fn = fn
        self.reads = tuple(reads)
        self.writes = tuple(writes)
        self.dma = dma
        self.key = key
        self.deps = set()
        self.signal = False
        self.count = None
        self.waits = []


class Prog:
    def __init__(self, nc):
        self.nc = nc
        self.ops = []

    def op(self, eng, fn, reads=(), writes=()):
        o = Op(eng, fn, reads, writes, False, None)
        o.idx = len(self.ops)
        self.ops.append(o)
        return o

    def dma(self, eng, fn, reads=(), writes=(), key=None):
        o = Op(eng, fn, reads, writes, True, key)
        o.idx = len(self.ops)
        self.ops.append(o)
        return o

    def fence(self, eng, reads):
        return self.op(eng, None, reads=reads)

    @staticmethod
    def _track(o):
        return ("dma:" + o.key) if o.dma else o.eng

    def analyze(self):
        ops = self.ops
        writers = {}
        readers = {}
        for o in ops:
            tr = self._track(o)
            deps = set()
            for r in o.reads:
                for t, w in writers.get(r, {}).items():
                    deps.add(w.idx)
                if r.startswith("PS") or r.startswith("PT"):
                    for t, rd in readers.get(r, {}).items():
                        if t != tr:
                            deps.add(rd.idx)
            for r in o.writes:
                for t, w in writers.get(r, {}).items():
                    if t == tr and o.eng == "pe" and not o.dma:
                        continue
                    deps.add(w.idx)
                for t, rd in readers.get(r, {}).items():
                    if t == tr and o.eng == "pe" and not o.dma:
                        continue
                    deps.add(rd.idx)
            deps.discard(o.idx)
            o.deps = deps
            for r in o.reads:
                readers.setdefault(r, {})[tr] = o
            for r in o.writes:
                writers[r] = {tr: o}
                readers[r] = {}
        for o in ops:
            for d in o.deps:
                ops[d].signal = True
        cnt = {}
        for o in ops:
            tr = self._track(o)
            if o.dma:
                cnt[tr] = cnt.get(tr, 0) + 16
                o.count = cnt[tr]
                o.signal = True
            elif o.signal:
                cnt[tr] = cnt.get(tr, 0) + 1
                o.count = cnt[tr]
        self.tracks = sorted(cnt.keys())
        waited = {e: {} for e in ENGS}
        for o in ops:
            need = {}
            for d in o.deps:
                dop = ops[d]
                tr = self._track(dop)
                need[tr] = max(need.get(tr, 0), dop.count)
            w = waited[o.eng]
            for tr, v in sorted(need.items()):
                if w.get(tr, 0) >= v:
                    continue
                w[tr] = v
                o.waits.append((tr, v))
        return self

    def emit(self):
        nc = self.nc
        sems = {tr: nc.alloc_semaphore(name="s_" + tr.replace(":", "_")) for tr in self.tracks}
        per_eng = {e: [o for o in self.ops if o.eng == e] for e in ENGS}
        track = self._track

        def run(engine, lst):
            for o in lst:
                for tr, v in o.waits:
                    engine.wait_ge(sems[tr], v)
                if o.fn is None:
                    continue
                ins = o.fn(engine)
                if o.signal:
                    ins.then_inc(sems[track(o)], 16 if o.dma else 1)

        with nc.Block() as block:
            @block.tensor
            def _(e):
                run(e, per_eng["pe"])

            @block.scalar
            def _(e):
                run(e, per_eng["act"])

            @block.vector
            def _(e):
                run(e, per_eng["dve"])

            @block.gpsimd
            def _(e):
                run(e, per_eng["pool"])

            @block.sync
            def _(e):
                run(e, per_eng["sp"])


def build_program(n_slots=4):
    nc = bass.Bass("TRN2", target_bir_lowering=False)
    NTOK = 2 * n_slots * TS
    NOWN = n_slots * TS

    xall = nc.dram_tensor("xall", [S, D], F32, kind="ExternalInput").ap()
    xown = nc.dram_tensor("xown", [4 * TS, D], F32, kind="ExternalInput").ap()
    w_in = nc.dram_tensor("w_in", [D, 6144], F32, kind="ExternalInput").ap()
    w_out = nc.dram_tensor("w_out", [D, D], F32, kind="ExternalInput").ap()
    wla_d = nc.dram_tensor("wla", [128, 8, 128], F32, kind="ExternalInput").ap()
    wlx_d = nc.dram_tensor("wlx", [128, 8, 128], F32, kind="ExternalInput").ap()
    prm_d = nc.dram_tensor("prm", [128, NPRM], F32, kind="ExternalInput").ap()
    ngb_d = nc.dram_tensor("ngb", [128, D], F32, kind="ExternalInput").ap()
    msk_d = nc.dram_tensor("masks", [2, 128, 8, TS], BF16, kind="ExternalInput").ap()
    out_d = nc.dram_tensor("out", [4 * TS, D], F32, kind="ExternalOutput").ap()
    kt_d = nc.dram_tensor("kt_scr", [NH, 128, S], BF16, kind="Internal").ap()
    v_d = nc.dram_tensor("v_scr", [S, 1024], BF16, kind="Internal").ap()
    wbf_d = nc.dram_tensor("wbf_scr", [48, 128, 2048], BF16, kind="Internal").ap()
    wbh_d = nc.dram_tensor("wbh_scr", [12, 128, 4096], BF16, kind="Internal").ap()

    sb = nc.alloc_sbuf_tensor
    ID = sb("ID", [128, 128], BF16)
    IDF = sb("IDF", [128, 128], F32)
    ONES = sb("ONES", [128, 128], BF16)
    NTRI = sb("NTRI", [128, 128], BF16)
    NONE = sb("NONE", [128, 128], BF16)
    PRM = sb("PRM", [128, NPRM], F32)
    DRV = sb("DRV", [128, 64], F32)
    NGB = sb("NGB", [128, D], F32)
    MSK = sb("MSK", [128, 8, TS], BF16)
    XS = [sb(f"XS{i}", [128, D], F32) for i in range(2)]
    XB0 = sb("XB0", [128, D], BF16)
    SS = sb("SS", [128, 8], F32)
    XT = [sb(f"XT{i}", [128, 16, TS], BF16) for i in range(2)]
    WBALL = sb("WBALL", [128, 4 * 2048], BF16)
    WLA = sb("WLA", [128, 8, 128], BF16)
    WLX = sb("WLX", [128, 8, 128], BF16)
    KST = [sb(f"KST{i}", [128, TS], BF16) for i in range(2)]
    VST = [sb(f"VST{i}", [128, 256], BF16) for i in range(2)]
    SQ = [sb(f"SQ{i}", [128, TS], BF16) for i in range(2)]
    RS = [sb("RS0", [128, TS], F32)]
    QT = sb("QT", [128, 8, TS], BF16)
    GSB = sb("GSB", [128, 8, TS], BF16)
    XB = [XB0[:, :], GSB[:, 4:8, :].rearrange("p a b -> p (a b)")]
    XBN = [["XB0"], ["GSB4", "GSB5", "GSB6", "GSB7"]]
    TG = [sb("TG0", [128, TS], F32)]
    XLW = [sb(f"XLW{i}", [128, TS + 3], F32) for i in range(2)]
    XC = [[sb(f"XC{q}_{i}", [128, TS], F32) for i in range(2)] for q in range(2)]
    XCB = [sb(f"XCB{i}", [128, TS], BF16) for i in range(2)]
    LT = [[sb(f"LT{i}_{j}", [128, TS], F32) for j in range(4)] for i in range(2)]
    EB0 = LT[1][3]
    EB = [EB0, EB0]
    YSEL = sb("YSEL", [128, 8, TS], BF16)
    HST = sb("HST", [128, 8], F32)
    HALO = sb("HALO", [128, 8, 4], F32)
    KTB = [sb(f"KTB{i}", [128, S], BF16) for i in range(2)]
    VB = [sb(f"VB{i}", [128, 32, 128], BF16) for i in range(2)]

    SPB = [sb(f"SPB{i}", [128, TS], BF16) for i in range(2)]
    CSB = [sb(f"CSB{i}", [128, TS], BF16) for i in range(2)]
    WWB = [sb(f"WWB{i}", [128, TS], BF16) for i in range(2)]
    YT = sb("YT", [128, 16, TS], BF16)

    PS = [nc.alloc_psum_tensor(f"PS{i}", [128, TS], F32) for i in range(8)]
    PT = [PS[6 + i][:].bitcast(BF16).rearrange("p (j c) -> p j c", c=128) for i in range(2)]

    def WB(i):
        return WBALL[:, i * 2048:(i + 1) * 2048].rearrange("p (k c) -> p k c", c=128)

    WOH = [WBALL[:, i * 4096:(i + 1) * 4096].rearrange("p (k c) -> p k c", c=256) for i in range(2)]
    WOHR = [["WB0", "WB1"], ["WB2", "WB3"]]
    WBR = ["WB0", "WB1", "WB2", "WB3"]

    P = Prog(nc)
    C_QG, C_KG, C_CW, C_CB, C_BA, C_BX, C_LAM, C_FL = 16, 17, 18, 50, 58, 66, 74, 82
    D_QGS, D_HBA, D_HBX, D_NSL, D_HNSL, D_TMP = 0, 8, 16, 24, 32, 40

    P.dma("sp", lambda e: e.dma_start(out=PRM[:], in_=prm_d), writes=["PRM"], key="PRM")
    P.dma("sp", lambda e: e.dma_start(out=NGB[:], in_=ngb_d), writes=["NGB"], key="NGB")
    P.dma("pool", lambda e: e.dma_start(out=WLA[:], in_=wla_d), writes=["WLA"], key="WLA")
    P.dma("pool", lambda e: e.dma_start(out=WLX[:], in_=wlx_d), writes=["WLX"], key="WLX")
    P.op("pool", lambda e: e.memset(IDF[:], 0.0), writes=["IDF"])
    P.op("pool", lambda e: e.affine_select(out=IDF[:], in_=IDF[:], pattern=[[-1, 128]],
                                           compare_op=ALU.not_equal, fill=1.0, base=0,
                                           channel_multiplier=1), reads=["IDF"], writes=["IDF"])
    P.op("dve", lambda e: e.tensor_copy(out=ID[:], in_=IDF[:]), reads=["IDF"], writes=["ID"])
    P.op("pool", lambda e: e.memset(ONES[:], 1.0 / 128.0), writes=["ONES"])
    P.op("pool", lambda e: e.memset(NONE[:], -1.0), writes=["NONE"])
    P.op("pool", lambda e: e.memset(IDF[:], -1.0), reads=["ID"], writes=["IDF"])
    P.op("pool", lambda e: e.affine_select(out=IDF[:], in_=IDF[:], pattern=[[-1, 128]],
                                           compare_op=ALU.is_ge, fill=0.0, base=0,
                                           channel_multiplier=1), reads=["IDF"], writes=["IDF"])
    P.op("dve", lambda e: e.tensor_copy(out=NTRI[:], in_=IDF[:]), reads=["IDF"], writes=["NTRI"])
    P.op("pool", lambda e: e.memset(HST[:], 0.0), writes=["HST"])
    P.op("pool", lambda e: e.memset(HALO[:], 0.0), writes=[f"HALO{c}" for c in range(8)])
    P.op("dve", lambda e: e.tensor_scalar(out=DRV[:, D_QGS:D_QGS + 1], in0=PRM[:, C_QG:C_QG + 1],
                                          scalar1=float(DH) ** -0.5, scalar2=None, op0=ALU.mult),
         reads=["PRM"], writes=["DRV"])
    P.op("dve", lambda e: e.tensor_scalar(out=DRV[:, D_HBA:D_HBA + 8], in0=PRM[:, C_BA:C_BA + 8],
                                          scalar1=0.5, scalar2=None, op0=ALU.mult),
         reads=["PRM"], writes=["DRV"])
    P.op("dve", lambda e: e.tensor_scalar(out=DRV[:, D_HBX:D_HBX + 8], in0=PRM[:, C_BX:C_BX + 8],
                                          scalar1=0.5, scalar2=None, op0=ALU.mult),
         reads=["PRM"], writes=["DRV"])
    P.op("act", lambda e: e.activation(out=DRV[:, D_TMP:D_TMP + 8], in_=PRM[:, C_LAM:C_LAM + 8],
                                       func=AF.Exp, scale=-1.0), reads=["PRM", "DRV"], writes=["DRVT"])
    P.op("act", lambda e: e.activation(out=DRV[:, D_TMP:D_TMP + 8], in_=DRV[:, D_TMP:D_TMP + 8],
                                       func=AF.Ln, bias=1.0), reads=["DRVT"], writes=["DRVT"])
    P.op("dve", lambda e: e.tensor_scalar(out=DRV[:, D_NSL:D_NSL + 8], in0=DRV[:, D_TMP:D_TMP + 8],
                                          scalar1=-8.0, scalar2=None, op0=ALU.mult),
         reads=["DRVT"], writes=["DRV"])
    P.op("dve", lambda e: e.tensor_scalar(out=DRV[:, D_HNSL:D_HNSL + 8], in0=DRV[:, D_TMP:D_TMP + 8],
                                          scalar1=-4.0, scalar2=None, op0=ALU.mult),
         reads=["DRVT"], writes=["DRV"])

    XSN = [[f"XS{i}_{j}" for j in range(8)] for i in range(2)]
    XCBN = [[f"XCB{i}_0", f"XCB{i}_1"] for i in range(2)]
    SQN = [[f"SQ{i}_0", f"SQ{i}_1"] for i in range(2)]
    VSLOT = [(VST[0][:, 0:256], "VST0"), (VST[1][:, 0:256], "VST1")]
    for i_ in range(2):
        for hf_ in range(2):
            VSLOT.append((XCB[i_][:, hf_ * 256:(hf_ + 1) * 256], XCBN[i_][hf_]))
            VSLOT.append((SQ[i_][:, hf_ * 256:(hf_ + 1) * 256], SQN[i_][hf_]))
    st = {"xs": 0, "vsl": 0, "xr": 0, "wb": 0, "ps": 0, "kst": 0, "vst": 0, "sq": 0, "tg": 0, "xl": 0, "xr": 0,
          "kvb": 0, "woh": 0}

    def rot(name, n):
        v = st[name]
        st[name] = (v + 1) % n
        return v

    def xt_names(xt):
        return [f"XT{xt}_{tb}_{hf}" for tb in range(4) for hf in (0, 1)]

    def norm_transpose(src, row0, xt):
        for _ in norm_transpose_gen(src, row0, xt):
            pass
        return xt_names(xt)

    def norm_transpose_gen(src, row0, xt, xb0_only=False):
        for tb in range(4):
            i = rot("xs", 2)
            xb = 0 if xb0_only else i
            r0 = row0 + tb * 128
            P.dma("sp", lambda e, i=i, r0=r0: e.dma_start(out=XS[i][:], in_=src[r0:r0 + 128, :]),
                  writes=XSN[i], key=f"XS{i}")
            P.op("act", lambda e, i=i, xb=xb: e.activation(out=XB[xb], in_=XS[i][:], func=AF.Square,
                                                    accum_out=SS[:, 2 * i:2 * i + 1]),
                 reads=XSN[i], writes=XBN[xb] + [f"SS{i}"])
            P.op("act", lambda e, i=i: e.activation(out=SS[:, 2 * i + 1:2 * i + 2], in_=SS[:, 2 * i:2 * i + 1],
                                                    func=AF.Ln, scale=1.0 / D, bias=EPS),
                 reads=[f"SS{i}"], writes=[f"SR{i}"])
            P.op("act", lambda e, i=i: e.activation(out=SS[:, 2 * i + 1:2 * i + 2],
                                                    in_=SS[:, 2 * i + 1:2 * i + 2],
                                                    func=AF.Exp, scale=-0.5),
                 reads=[f"SR{i}"], writes=[f"SR{i}"])
            P.op("dve", lambda e, i=i, xb=xb: e.scalar_tensor_tensor(out=XB[xb], in0=XS[i][:],
                                                              scalar=SS[:, 2 * i + 1:2 * i + 2],
                                                              in1=NGB[:], op0=ALU.mult, op1=ALU.mult),
                 reads=XSN[i] + [f"SR{i}", "NGB"], writes=XBN[xb])
            for hf in range(2):
                for j in range(8):
                    dc = hf * 8 + j
                    P.op("pe", lambda e, xb=xb, hf=hf, j=j, dc=dc: e.transpose(
                        out=PT[hf][:, j, :], in_=XB[xb][:, dc * 128:(dc + 1) * 128], identity=ID[:]),
                         reads=XBN[xb] + ["ID"], writes=[f"PS{6 + hf}"])
            P.op("act", lambda e, tb=tb, xt=xt: e.activation(
                out=XT[xt][:, 0:8, tb * 128:(tb + 1) * 128], in_=PT[0], func=AF.Copy),
                 reads=["PS6"], writes=[f"XT{xt}_{tb}_0"])
            P.op("dve", lambda e, tb=tb, xt=xt: e.tensor_copy(
                out=XT[xt][:, 8:16, tb * 128:(tb + 1) * 128], in_=PT[1]),
                 reads=["PS7"], writes=[f"XT{xt}_{tb}_1"])
            yield tb

    cur = {"s": 0}

    def load_wchunk(wsrc, col0):
        i = rot("wb", 4)
        cid = col0 // 128
        flat = WBALL[:, i * 2048:(i + 1) * 2048]
        if cur["s"] == 0:
            P.dma("pool", lambda e, i=i, col0=col0: e.dma_start(
                out=WB(i), in_=wsrc[:, col0:col0 + 128].rearrange("(k p) c -> p k c", p=128)),
                  writes=[WBR[i]], key=WBR[i])
            P.dma("sp", lambda e, flat=flat, cid=cid: e.dma_start(out=wbf_d[cid], in_=flat),
                  reads=[WBR[i]], writes=[f"WBF{cid}"], key=f"WBS{i}")
        else:
            P.dma("pool", lambda e, flat=flat, cid=cid: e.dma_start(out=flat, in_=wbf_d[cid]),
                  reads=[f"WBF{cid}"], writes=[WBR[i]], key=WBR[i])
        return i

    def load_woh(wsrc, col0, hid):
        i = rot("woh", 2)
        flat = WBALL[:, i * 4096:(i + 1) * 4096]
        if cur["s"] == 0:
            P.dma("pool", lambda e, col0=col0, i=i: e.dma_start(
                out=WOH[i], in_=wsrc[:, col0:col0 + 256].rearrange("(k p) c -> p k c", p=128)),
                  writes=WOHR[i], key=f"WOH{i}")
            P.dma("sp", lambda e, flat=flat, hid=hid: e.dma_start(out=wbh_d[hid], in_=flat),
                  reads=WOHR[i], writes=[f"WBH{hid}"], key=f"WHS{i}")
        else:
            P.dma("pool", lambda e, flat=flat, hid=hid: e.dma_start(out=flat, in_=wbh_d[hid]),
                  reads=[f"WBH{hid}"], writes=WOHR[i], key=f"WOH{i}")
        return i

    def proj_fm(wi, xt, xtn, b=None):
        if b is None:
            b = rot("ps", 4)
        for dc in range(16):
            P.op("pe", lambda e, b=b, wi=wi, xt=xt, dc=dc: e.matmul(
                PS[b][:], lhsT=WB(wi)[:, dc, :], rhs=XT[xt][:, dc, :], start=(dc == 0), stop=(dc == 15)),
                 reads=[WBR[wi]] + xtn, writes=[f"PS{b}"])
        return b

    def qk_norm(b, gain_ap, gain_res, dst_fn, dst_res):
        i = 0
        P.op("act", lambda e, b=b, i=i: e.activation(out=SQ[i][:], in_=PS[b][:], func=AF.Square),
             reads=[f"PS{b}"], writes=SQN[i])
        sbank = 4
        P.op("pe", lambda e, i=i, sbank=sbank: e.matmul(PS[sbank][:], lhsT=ONES[:], rhs=SQ[i][:],
                                                        start=True, stop=True),
             reads=["ONES"] + SQN[i], writes=[f"PS{sbank}"])
        P.op("act", lambda e, i=i, sbank=sbank: e.activation(out=RS[i][:], in_=PS[sbank][:], func=AF.Ln,
                                                             bias=EPS),
             reads=[f"PS{sbank}"], writes=[f"RS{i}"])
        P.op("act", lambda e, i=i: e.activation(out=RS[i][:], in_=RS[i][:], func=AF.Exp, scale=-0.5),
             reads=[f"RS{i}"], writes=[f"RS{i}"])
        P.op("dve", lambda e, b=b, i=i: e.scalar_tensor_tensor(
            out=dst_fn(), in0=PS[b][:], scalar=gain_ap, in1=RS[i][:], op0=ALU.mult, op1=ALU.mult),
             reads=[f"PS{b}", f"RS{i}", gain_res], writes=[dst_res])

    def k_norm_sq(b, ti):
        P.op("act", lambda e, b=b, ti=ti: e.activation(out=SQ[ti][:], in_=PS[b][:], func=AF.Square),
             reads=[f"PS{b}"], writes=SQN[ti])

    def k_norm_rest(b, ti, dst, dst_res):
        P.op("pe", lambda e, ti=ti: e.matmul(PS[4][:], lhsT=ONES[:], rhs=SQ[ti][:], start=True, stop=True),
             reads=["ONES"] + SQN[ti], writes=["PS4"])
        P.op("act", lambda e: e.activation(out=RS[0][:], in_=PS[4][:], func=AF.Ln, bias=EPS),
             reads=["PS4"], writes=["RS0"])
        P.op("act", lambda e: e.activation(out=RS[0][:], in_=RS[0][:], func=AF.Exp, scale=-0.5),
             reads=["RS0"], writes=["RS0"])
        P.op("dve", lambda e, b=b, dst=dst: e.scalar_tensor_tensor(
            out=dst[:], in0=PS[b][:], scalar=PRM[:, C_KG:C_KG + 1], in1=RS[0][:], op0=ALU.mult, op1=ALU.mult),
             reads=[f"PS{b}", "RS0", "PRM"], writes=[dst_res])

    def lru_front(c, banks, par):
        cw = lambda k: PRM[:, C_CW + 4 * c + k:C_CW + 4 * c + k + 1]
        cb = PRM[:, C_CB + c:C_CB + c + 1]
        for ti, b in enumerate(banks):
            xc = XC[par][ti]
            xcn = f"XC{par}_{ti}"
            P.op("dve", lambda e, b=b, ti=ti: e.tensor_copy(out=XLW[ti][:, 3:TS + 3], in_=PS[b][:]),
                 reads=[f"PS{b}"], writes=[f"XLW{ti}"])
            P.op("dve", lambda e, ti=ti: e.tensor_copy(out=XLW[ti][:, 0:3], in_=HALO[:, c, 0:3]),
                 reads=[f"HALO{c}"], writes=[f"XLWh{ti}"])
            P.op("dve", lambda e, ti=ti: e.tensor_copy(out=HALO[:, c, 0:3], in_=XLW[ti][:, TS:TS + 3]),
                 reads=[f"XLW{ti}", f"XLWh{ti}"], writes=[f"HALO{c}"])
            P.op("dve", lambda e, ti=ti, xc=xc: e.tensor_scalar(
                out=xc[:], in0=XLW[ti][:, 3:TS + 3], scalar1=cw(3), scalar2=cb, op0=ALU.mult, op1=ALU.add),
                 reads=[f"XLW{ti}", "PRM"], writes=[xcn])
            for k in (0, 1, 2):
                P.op("dve", lambda e, ti=ti, k=k, xc=xc: e.scalar_tensor_tensor(
                    out=xc[:], in0=XLW[ti][:, k:TS + k], scalar=cw(k), in1=xc[:],
                    op0=ALU.mult, op1=ALU.add),
                     reads=[f"XLW{ti}", f"XLWh{ti}", xcn, "PRM"], writes=[xcn])
            P.op("act", lambda e, ti=ti, xc=xc: e.activation(out=XCB[ti][:], in_=xc[:], func=AF.Copy),
                 reads=[xcn], writes=XCBN[ti])

    def lru_mid_gates(c):
        dv = lambda col: DRV[:, col + c:col + c + 1]
        for ti in range(2):
            L = LT[ti]
            LR = [f"LT{ti}_{j}" for j in range(4)]
            P.op("pe", lambda e, ti=ti: e.matmul(PS[5][:], lhsT=WLA[:, c, :], rhs=XCB[ti][:],
                                                 start=True, stop=True),
                 reads=["WLA"] + XCBN[ti], writes=["PS5"])
            P.op("act", lambda e, L=L: e.activation(out=L[0][:], in_=PS[5][:], func=AF.Tanh, scale=0.5,
                                                    bias=dv(D_HBA)),
                 reads=["PS5", "DRV"], writes=[LR[0]])
            P.op("pe", lambda e, ti=ti: e.matmul(PS[4][:], lhsT=WLX[:, c, :], rhs=XCB[ti][:],
                                                 start=True, stop=True),
                 reads=["WLX"] + XCBN[ti], writes=["PS4"])
            P.op("act", lambda e, L=L: e.activation(out=L[1][:], in_=PS[4][:], func=AF.Tanh, scale=0.5,
                                                    bias=dv(D_HBX)),
                 reads=["PS4", "DRV"], writes=[LR[1]])

    def lru_mid_rest(c):
        dv = lambda col: DRV[:, col + c:col + c + 1]
        for ti in range(2):
            L = LT[ti]
            LR = [f"LT{ti}_{j}" for j in range(4)]
            P.op("act", lambda e, L=L: e.activation(out=L[2][:], in_=L[0][:], func=AF.Exp, scale=dv(D_HNSL),
                                                    bias=dv(D_HNSL)),
                 reads=[LR[0], "DRV"], writes=[LR[2]])
            P.op("act", lambda e, L=L: e.activation(out=L[3][:], in_=L[0][:], func=AF.Exp, scale=dv(D_NSL),
                                                    bias=dv(D_NSL)),
                 reads=[LR[0], "DRV"], writes=[LR[3]])
        for ti in range(2):
            L = LT[ti]
            LR = [f"LT{ti}_{j}" for j in range(4)]
            P.op("act", lambda e, L=L: e.activation(out=L[3][:], in_=L[3][:], func=AF.Ln, scale=-0.25, bias=0.25),
                 reads=[LR[3]], writes=[LR[3]])
        for ti in range(2):
            L = LT[ti]
            LR = [f"LT{ti}_{j}" for j in range(4)]
            P.op("act", lambda e, L=L: e.activation(out=L[3][:], in_=L[3][:], func=AF.Exp, scale=0.5),
                 reads=[LR[3]], writes=[LR[3]])

    def lru_tail(c, par, fl0, fl1):
        for ti in range(2):
            L = LT[ti]
            LR = [f"LT{ti}_{j}" for j in range(4)]
            xc = XC[par][ti]
            xcn = f"XC{par}_{ti}"
            P.op("dve", lambda e, L=L, xc=xc: e.scalar_tensor_tensor(out=L[1][:], in0=L[1][:], scalar=1.0,
                                                                     in1=xc[:], op0=ALU.add, op1=ALU.mult),
                 reads=[LR[1], xcn], writes=[LR[1]])
            P.op("dve", lambda e, L=L: e.tensor_tensor(out=L[1][:], in0=L[1][:], in1=L[3][:], op=ALU.mult),
                 reads=[LR[1], LR[3]], writes=[LR[1]])
            if ti == 0:
                P.op("dve", lambda e, L=L: e.tensor_tensor_scan(
                    out=L[0][:], data0=L[2][:], data1=L[1][:], initial=HST[:, c:c + 1],
                    op0=ALU.mult, op1=ALU.add),
                     reads=[LR[2], LR[1], f"HST{c}", "HST"], writes=[LR[0]])
                P.op("dve", lambda e, L=L: e.tensor_scalar(
                    out=YSEL[:, c, :], in0=L[0][:], scalar1=fl0, scalar2=None, op0=ALU.mult),
                     reads=[LR[0], "PRM"], writes=[f"YSEL{c}"])
            else:
                P.op("dve", lambda e, L=L: e.tensor_tensor_scan(
                    out=L[0][:], data0=L[2][:], data1=L[1][:], initial=LT[0][0][:, TS - 1:TS],
                    op0=ALU.mult, op1=ALU.add),
                     reads=[LR[2], LR[1], "LT0_0"], writes=[LR[0]])
                P.op("dve", lambda e, L=L: e.scalar_tensor_tensor(
                    out=YSEL[:, c, :], in0=L[0][:], scalar=fl1, in1=YSEL[:, c, :],
                    op0=ALU.mult, op1=ALU.add),
                     reads=[LR[0], "PRM", f"YSEL{c}"], writes=[f"YSEL{c}"])
                P.op("dve", lambda e, L=L: e.tensor_copy(out=HST[:, c:c + 1], in_=L[0][:, TS - 1:TS]),
                     reads=[LR[0]], writes=[f"HST{c}"])

    out_res = []
    for s in range(n_slots):
        cur["s"] = s
        A, Bt = 2 * s, 2 * s + 1
        fl0 = PRM[:, C_FL + 2 * s:C_FL + 2 * s + 1]
        fl1 = PRM[:, C_FL + 2 * s + 1:C_FL + 2 * s + 2]
        P.dma("sp", lambda e, s=s: e.dma_start(out=MSK[:], in_=msk_d[s % 2]), writes=["MSK"], key="MSK")
        if s == 0:
            xtnA = norm_transpose(xall, A * TS, 0)
            xtnB = norm_transpose(xall, Bt * TS, 1)
        else:
            xtnA, xtnB = xt_names(0), xt_names(1)
        tiles = ((A, 0, xtnA), (Bt, 1, xtnB))
        def lru_proj(c):
            wi = load_wchunk(w_in, 4096 + c * 128)
            return [proj_fm(wi, xt, xtn, b=ti) for ti, (T, xt, xtn) in enumerate(tiles)]

        banks = lru_proj(0)
        lru_front(0, banks, 0)
        pend = None

        def k_finish(pend):
            cp, kb_ = pend
            for ti, (T, xt, xtn) in enumerate(tiles):
                k = rot("kst", 2)
                k_norm_rest(kb_[ti], ti, KST[k], f"KST{k}")
                P.dma("sp", lambda e, k=k, h=cp, T=T: e.dma_start(out=kt_d[h, :, T * TS:(T + 1) * TS],
                                                                  in_=KST[k][:]),
                      reads=[f"KST{k}"], writes=[f"KTd{cp}_{T}"], key=f"KST{k}")

        for c in range(8):
            lru_mid_gates(c)
            lru_mid_rest(c)
            if c + 1 < 8:
                banks = lru_proj(c + 1)
            if pend is not None:
                k_finish(pend)
            wi = load_wchunk(w_in, 1024 + c * 128)
            kbanks = []
            for ti, (T, xt, xtn) in enumerate(tiles):
                b = proj_fm(wi, xt, xtn, b=(2 if c % 2 == 0 else 6) + ti)
                k_norm_sq(b, ti)
                kbanks.append(b)
            pend = (c, kbanks)
            if c + 1 < 8:
                lru_front(c + 1, banks, (c + 1) % 2)
            lru_tail(c, c % 2, fl0, fl1)
        k_finish(pend)
        for grp in range(4):
            wh = load_woh(w_in, 2048 + grp * 256, grp)
            for (T, xt, xtn) in tiles:
                for tb in range(4):
                    b = rot("ps", 4)
                    for dc in range(16):
                        P.op("pe", lambda e, b=b, xt=xt, dc=dc, tb=tb, wh=wh: e.matmul(
                            PS[b][:, 0:256], lhsT=XT[xt][:, dc, tb * 128:(tb + 1) * 128], rhs=WOH[wh][:, dc, :],
                            start=(dc == 0), stop=(dc == 15)),
                             reads=WOHR[wh] + xtn, writes=[f"PS{b}"])
                    vi = rot("vsl", len(VSLOT))
                    vap, vnm = VSLOT[vi]
                    P.op("act", lambda e, b=b, vap=vap: e.activation(out=vap, in_=PS[b][:, 0:256], func=AF.Copy),
                         reads=[f"PS{b}"], writes=[vnm])
                    r0 = T * TS + tb * 128
                    P.dma("sp", lambda e, vap=vap, r0=r0, grp=grp: e.dma_start(
                        out=v_d[r0:r0 + 128, grp * 256:(grp + 1) * 256], in_=vap),
                          reads=[vnm], writes=[f"Vd{grp}_{T}_{tb}"], key=f"VS_{vnm}")
        xtnO = norm_transpose(xown, s * TS, 0)
        for h in range(NH):
            wi = load_wchunk(w_in, h * 128)
            b = proj_fm(wi, 0, xtnO)
            qk_norm(b, DRV[:, D_QGS:D_QGS + 1], "DRV", lambda h=h: QT[:, h, :], f"QT{h}")
        for h in range(NH):
            wi = load_wchunk(w_in, 3072 + h * 128)
            b = proj_fm(wi, 0, xtnO)
            g = 0
            P.op("act", lambda e, b=b, g=g: e.activation(out=TG[g][:], in_=PS[b][:], func=AF.Tanh, scale=0.5),
                 reads=[f"PS{b}"], writes=[f"TG{g}"])
            P.op("dve", lambda e, b=b, g=g, h=h: e.scalar_tensor_tensor(
                out=GSB[:, h, :], in0=TG[g][:], scalar=1.0, in1=PS[b][:], op0=ALU.add, op1=ALU.mult),
                 reads=[f"TG{g}", f"PS{b}"], writes=[f"GSB{h}"])
        for c in range(8):
            wi = load_wchunk(w_in, 5120 + c * 128)
            b = proj_fm(wi, 0, xtnO)
            g = 0
            P.op("act", lambda e, b=b, g=g: e.activation(out=TG[g][:], in_=PS[b][:], func=AF.Tanh, scale=0.5),
                 reads=[f"PS{b}"], writes=[f"TG{g}"])
            P.op("dve", lambda e, b=b, g=g: e.scalar_tensor_tensor(
                out=TG[g][:], in0=TG[g][:], scalar=1.0, in1=PS[b][:], op0=ALU.add, op1=ALU.mult),
                 reads=[f"TG{g}", f"PS{b}"], writes=[f"TG{g}"])
            P.op("dve", lambda e, g=g, c=c: e.scalar_tensor_tensor(
                out=YT[:, 8 + c, :], in0=TG[g][:], scalar=0.5, in1=YSEL[:, c, :], op0=ALU.mult, op1=ALU.mult),
                 reads=[f"TG{g}", f"YSEL{c}"], writes=[f"YT{8 + c}"])
        nkb = 4 * (2 * s + 2)
        ext = nkb * 128
        pre = None
        if s + 1 < n_slots:
            def _pre(s=s):
                yield from norm_transpose_gen(xall, (2 * s + 2) * TS, 0, xb0_only=True)
                yield from norm_transpose_gen(xall, (2 * s + 3) * TS, 1, xb0_only=True)
            pre = _pre()
        for h in range(NH):
            kv = rot("kvb", 2)
            P.dma("sp", lambda e, kv=kv, h=h, ext=ext: e.dma_start(out=KTB[kv][:, 0:ext], in_=kt_d[h, :, 0:ext]),
                  reads=[f"KTd{h}_{T}" for T in range(2 * s + 2)], writes=[f"KTB{kv}"], key=f"KTB{kv}")
            for part in range(nkb // 8):
                P.dma("sp", lambda e, kv=kv, h=h, part=part: e.dma_start(
                    out=VB[kv][:, part * 8:(part + 1) * 8, :],
                    in_=v_d[part * 1024:(part + 1) * 1024, h * 128:(h + 1) * 128].rearrange(
                        "(k p) d -> p k d", p=128)),
                      reads=[f"Vd{h // 2}_{T}_{tb}" for T in (2 * part, 2 * part + 1) for tb in range(4)],
                      writes=[f"VB{kv}_{part}"], key=f"VB{kv}_{part}")
            ob = 4 + (h % 2)
            units = list(range(nkb - 1, -1, -1))
            n = len(units)

            def mm_z(u, kv=kv, h=h, nkb=nkb, units=units):
                kb = units[u]
                z = u % 2
                P.op("pe", lambda e, kb=kb, z=z, kv=kv, h=h: e.matmul(
                    PS[z][:], lhsT=KTB[kv][:, kb * 128:(kb + 1) * 128], rhs=QT[:, h, :], start=True, stop=True),
                     reads=[f"KTB{kv}", f"QT{h}"], writes=[f"PS{z}"])
                P.op("act", lambda e, z=z: e.activation(out=EB[z][:], in_=PS[z][:], func=AF.Exp),
                     reads=[f"PS{z}"], writes=["LT1_3"])
                P.op("act", lambda e, z=z: e.activation(out=SPB[z][:], in_=EB[z][:], func=AF.Ln, bias=1.0),
                     reads=["LT1_3"], writes=[f"SPB{z}"])
                j = kb - (nkb - 8)
                if j >= 0:
                    P.op("dve", lambda e, z=z, j=j: e.tensor_tensor(out=SPB[z][:], in0=SPB[z][:],
                                                                    in1=MSK[:, j, :], op=ALU.mult),
                         reads=[f"SPB{z}", "MSK"], writes=[f"SPB{z}"])

            def mm_b(u, kv=kv, h=h, nkb=nkb, units=units, n=n):
                kb = units[u]
                z = u % 2
                bb = 2 + z
                last = (u == 0)
                P.op("pe", lambda e, kb=kb, bb=bb, kv=kv, h=h: e.matmul(
                    PS[bb][:], lhsT=KTB[kv][:, kb * 128:(kb + 1) * 128], rhs=QT[:, h, :], start=True, stop=False),
                     reads=[f"KTB{kv}", f"QT{h}"], writes=[f"PS{bb}"])
                P.op("pe", lambda e, z=z, bb=bb, last=last: e.matmul(PS[bb][:], lhsT=NTRI[:], rhs=SPB[z][:],
                                                                     start=False, stop=last),
                     reads=["NTRI", f"SPB{z}"], writes=[f"PS{bb}"])
                if not last:
                    P.op("pe", lambda e, z=z, bb=bb: e.matmul(PS[bb][:], lhsT=NONE[:], rhs=CSB[z][:],
                                                              start=False, stop=True),
                         reads=["NONE", f"CSB{z}"], writes=[f"PS{bb}"])
                P.op("act", lambda e, z=z, bb=bb: e.activation(out=WWB[z][:], in_=PS[bb][:], func=AF.Exp),
                     reads=[f"PS{bb}"], writes=[f"WWB{z}"])
                j = kb - (nkb - 8)
                if j >= 0:
                    P.op("dve", lambda e, z=z, j=j: e.tensor_tensor(out=WWB[z][:], in0=WWB[z][:],
                                                                    in1=MSK[:, j, :], op=ALU.mult),
                         reads=[f"WWB{z}", "MSK"], writes=[f"WWB{z}"])
                if u + 1 < n:
                    if u == 0:
                        P.op("dve", lambda e, z=z: e.tensor_copy(out=CSB[1 - z][:], in_=SPB[z][:]),
                             reads=[f"SPB{z}"], writes=[f"CSB{1 - z}"])
                    else:
                        P.op("dve", lambda e, z=z: e.tensor_tensor(out=CSB[1 - z][:], in0=CSB[z][:],
                                                                   in1=SPB[z][:], op=ALU.add),
                             reads=[f"SPB{z}", f"CSB{z}"], writes=[f"CSB{1 - z}"])

            def mm_o(u, kv=kv, ob=ob, units=units, n=n):
                kb = units[u]
                z = u % 2
                P.op("pe", lambda e, kb=kb, z=z, u=u, kv=kv, ob=ob, n=n: e.matmul(
                    PS[ob][:], lhsT=VB[kv][:, kb, :], rhs=WWB[z][:], start=(u == 0), stop=(u == n - 1)),
                     reads=[f"VB{kv}_{kb // 8}", f"WWB{z}"], writes=[f"PS{ob}"])

            for i in range(-1, n + 1):
                if 0 <= i + 1 < n:
                    mm_z(i + 1)
                if 0 <= i < n:
                    mm_b(i)
                if 0 <= i - 1 < n:
                    mm_o(i - 1)
            P.op("dve", lambda e, h=h, ob=ob: e.scalar_tensor_tensor(
                out=YT[:, h, :], in0=PS[ob][:], scalar=0.5, in1=GSB[:, h, :], op0=ALU.mult, op1=ALU.mult),
                 reads=[f"PS{ob}", f"GSB{h}"], writes=[f"YT{h}"])
            if pre is not None:
                next(pre, None)
        if pre is not None:
            for _ in pre:
                pass
        ytn = [f"YT{k}" for k in range(16)]
        items = [(grp, tb) for grp in range(8) for tb in range(4)]

        def xr_load(n):
            grp, tb = items[n]
            j = n % 8
            r0 = s * TS + tb * 128
            P.dma("pool", lambda e, j=j, r0=r0, grp=grp: e.dma_start(
                out=XS[0][:, j * 256:(j + 1) * 256], in_=xown[r0:r0 + 128, grp * 256:(grp + 1) * 256]),
                  writes=[XSN[0][j]], key=f"XR{j}")

        for n in range(8):
            xr_load(n)
        wh = None
        wh_next = load_woh(w_out, 0, 4)
        for n, (grp, tb) in enumerate(items):
            if tb == 0:
                wh = wh_next
                if grp + 1 < 8:
                    wh_next = load_woh(w_out, (grp + 1) * 256, 4 + grp + 1)
            b = rot("ps", 4)
            for ec in range(16):
                P.op("pe", lambda e, b=b, ec=ec, tb=tb, wh=wh: e.matmul(
                    PS[b][:, 0:256], lhsT=YT[:, ec, tb * 128:(tb + 1) * 128], rhs=WOH[wh][:, ec, :],
                    start=(ec == 0), stop=(ec == 15)),
                     reads=WOHR[wh] + ytn, writes=[f"PS{b}"])
            j = n % 8
            xin = XS[0][:, j * 256:(j + 1) * 256]
            xout = XS[1][:, j * 256:(j + 1) * 256]
            P.op("dve", lambda e, b=b, xin=xin, xout=xout: e.tensor_tensor(out=xout, in0=PS[b][:, 0:256], in1=xin,
                                                                         op=ALU.add),
                 reads=[f"PS{b}", XSN[0][j]], writes=[XSN[1][j]])
            nm = f"out_{s}_{grp}_{tb}"
            r0 = s * TS + tb * 128
            P.dma("sp", lambda e, xout=xout, r0=r0, grp=grp: e.dma_start(
                out=out_d[r0:r0 + 128, grp * 256:(grp + 1) * 256], in_=xout),
                  reads=[XSN[1][j]], writes=[nm], key=f"OS{j}")
            out_res.append(nm)
            if n + 8 < len(items):
                xr_load(n + 8)
    P.fence("sp", out_res)
    P.analyze().emit()
    return nc


def _own_tiles(p):
    return [2 * s + ((s + p) % 2) for s in range(4)]


def _masks(p):
    r = np.arange(128)[:, None]
    c = np.arange(TS)[None, :]
    m = np.zeros((2, 128, 8, TS), np.float32)
    for q in range(2):
        sel = (q + p) % 2
        for j in range(4):
            diag = ((j * 128 + r) < c).astype(np.float32)
            if sel == 0:
                m[q, :, j, :] = diag
                m[q, :, 4 + j, :] = 0.0
            else:
                m[q, :, j, :] = 1.0
                m[q, :, 4 + j, :] = diag
    return m.astype(ml_dtypes.bfloat16)


_NC_CACHE = {}


def kernel(x, norm_gain, w_in, q_norm_gain, k_norm_gain, conv_w, conv_b,
           lru_w_a, lru_b_a, lru_w_x, lru_b_x, lru_lambda, w_out):
    x = np.asarray(x, np.float32)
    f = lambda a: np.ascontiguousarray(np.asarray(a, np.float32))
    w_in0 = f(w_in[0])
    w_out0 = f(w_out[0])
    wla = f(np.transpose(np.asarray(lru_w_a[0]), (1, 0, 2)))
    wlx = f(np.transpose(np.asarray(lru_w_x[0]), (1, 0, 2)))
    ngb = f(np.broadcast_to(np.asarray(norm_gain[0])[None, :], (128, D)))
    col = lambda v: np.asarray(v, np.float32).reshape(8, 128).T
    in_maps = []
    for core in range(8):
        b, p = core // 2, core % 2
        prm = np.zeros((128, NPRM), np.float32)
        prm[:, 0:16] = np.asarray(norm_gain[0], np.float32).reshape(16, 128).T
        prm[:, 16] = np.asarray(q_norm_gain[0], np.float32)
        prm[:, 17] = np.asarray(k_norm_gain[0], np.float32)
        cw = np.asarray(conv_w[0], np.float32)
        for c in range(8):
            for k in range(4):
                prm[:, 18 + 4 * c + k] = cw[k, c * 128:(c + 1) * 128]
        prm[:, 50:58] = col(conv_b[0])
        prm[:, 58:66] = col(lru_b_a[0])
        prm[:, 66:74] = col(lru_b_x[0])
        prm[:, 74:82] = col(lru_lambda[0])
        own = _own_tiles(p)
        for s in range(4):
            sel = (s + p) % 2
            prm[:, 82 + 2 * s + 0] = 1.0 if sel == 0 else 0.0
            prm[:, 82 + 2 * s + 1] = 1.0 if sel == 1 else 0.0
        xb = x[b]
        xown = np.ascontiguousarray(
            np.concatenate([xb[t * TS:(t + 1) * TS] for t in own], axis=0))
        in_maps.append({
            "xall": np.ascontiguousarray(xb), "xown": xown, "w_in": w_in0, "w_out": w_out0,
            "wla": wla, "wlx": wlx, "prm": prm, "ngb": ngb, "masks": _masks(p),
        })
    if "nc" not in _NC_CACHE:
        _NC_CACHE["nc"] = build_program()
    nc = _NC_CACHE["nc"]
    res = run_bass_kernel_spmd(nc, in_maps, core_ids=list(range(8)))
    out = np.empty((B, S, D), np.float32)
    for core in range(8):
        b, p = core // 2, core % 2
        o = np.asarray(res.results[core]["out"], np.float32)
        for s, t in enumerate(_own_tiles(p)):
            out[b, t * TS:(t + 1) * TS] = o[s * TS:(s + 1) * TS]
    return out
```

```python
import numpy as np
import ml_dtypes
import concourse.bass as bass
import concourse.mybir as mybir
from concourse.bass_utils import run_bass_kernel_spmd

F32 = mybir.dt.float32
BF16 = mybir.dt.bfloat16
AF = mybir.ActivationFunctionType
ALU = mybir.AluOpType

D = 2048
S = 4096
B = 4
NT = 8
TS = 512
DH = 128
NH = 8
EPS = 1e-6
NPRM = 96

ENGS = ("pe", "act", "dve", "pool", "sp")


class Op:
    __slots__ = ("eng", "fn", "reads", "writes", "dma", "key", "deps", "idx",
                 "signal", "count", "waits")

    def __init__(self, eng, fn, reads, writes, dma, key):
        self.eng = eng
        self.fn = fn
        self.reads = tuple(reads)
        self.writes = tuple(writes)
        self.dma = dma
        self.key = key
        self.deps = set()
        self.signal = False
        self.count = None
        self.waits = []


class Prog:
    def __init__(self, nc):
        self.nc = nc
        self.ops = []

    def op(self, eng, fn, reads=(), writes=()):
        o = Op(eng, fn, reads, writes, False, None)
        o.idx = len(self.ops)
        self.ops.append(o)
        return o

    def dma(self, eng, fn, reads=(), writes=(), key=None):
        o = Op(eng, fn, reads, writes, True, key)
        o.idx = len(self.ops)
        self.ops.append(o)
        return o

    def fence(self, eng, reads):
        return self.op(eng, None, reads=reads)

    @staticmethod
    def _track(o):
        return ("dma:" + o.key) if o.dma else o.eng

    def analyze(self):
        ops = self.ops
        writers = {}
        readers = {}
        for o in ops:
            tr = self._track(o)
            deps = set()
            for r in o.reads:
                for t, w in writers.get(r, {}).items():
                    deps.add(w.idx)
                if r.startswith("PS") or r.startswith("PT"):
                    for t, rd in readers.get(r, {}).items():
                        if t != tr:
                            deps.add(rd.idx)
            for r in o.writes:
                for t, w in writers.get(r, {}).items():
                    if t == tr and o.eng == "pe" and not o.dma:
                        continue
                    deps.add(w.idx)
                for t, rd in readers.get(r, {}).items():
                    if t == tr and o.eng == "pe" and not o.dma:
                        continue
                    deps.add(rd.idx)
            deps.discard(o.idx)
            o.deps = deps
            for r in o.reads:
                readers.setdefault(r, {})[tr] = o
            for r in o.writes:
                writers[r] = {tr: o}
                readers[r] = {}
        for o in ops:
            for d in o.deps:
                ops[d].signal = True
        cnt = {}
        for o in ops:
            tr = self._track(o)
            if o.dma:
                cnt[tr] = cnt.get(tr, 0) + 16
                o.count = cnt[tr]
                o.signal = True
            elif o.signal:
                cnt[tr] = cnt.get(tr, 0) + 1
                o.count = cnt[tr]
        self.tracks = sorted(cnt.keys())
        waited = {e: {} for e in ENGS}
        for o in ops:
            need = {}
            for d in o.deps:
                dop = ops[d]
                tr = self._track(dop)
                need[tr] = max(need.get(tr, 0), dop.count)
            w = waited[o.eng]
            for tr, v in sorted(need.items()):
                if w.get(tr, 0) >= v:
                    continue
                w[tr] = v
                o.waits.append((tr, v))
        return self

    def emit(self):
        nc = self.nc
        sems = {tr: nc.alloc_semaphore(name="s_" + tr.replace(":", "_")) for tr in self.tracks}
        per_eng = {e: [o for o in self.ops if o.eng == e] for e in ENGS}
        track = self._track

        def run(engine, lst):
            for o in lst:
                for tr, v in o.waits:
                    engine.wait_ge(sems[tr], v)
                if o.fn is None:
                    continue
                ins = o.fn(engine)
                if o.signal:
                    ins.then_inc(sems[track(o)], 16 if o.dma else 1)

        with nc.Block() as block:
            @block.tensor
            def _(e):
                run(e, per_eng["pe"])

            @block.scalar
            def _(e):
                run(e, per_eng["act"])

            @block.vector
            def _(e):
                run(e, per_eng["dve"])

            @block.gpsimd
            def _(e):
                run(e, per_eng["pool"])

            @block.sync
            def _(e):
                run(e, per_eng["sp"])


def build_program(n_slots=4):
    nc = bass.Bass("TRN2", target_bir_lowering=False)
    NTOK = 2 * n_slots * TS
    NOWN = n_slots * TS

    xall = nc.dram_tensor("xall", [S, D], F32, kind="ExternalInput").ap()
    xown = nc.dram_tensor("xown", [4 * TS, D], F32, kind="ExternalInput").ap()
    w_in = nc.dram_tensor("w_in", [D, 6144], F32, kind="ExternalInput").ap()
    w_out = nc.dram_tensor("w_out", [D, D], F32, kind="ExternalInput").ap()
    wla_d = nc.dram_tensor("wla", [128, 8, 128], F32, kind="ExternalInput").ap()
    wlx_d = nc.dram_tensor("wlx", [128, 8, 128], F32, kind="ExternalInput").ap()
    prm_d = nc.dram_tensor("prm", [128, NPRM], F32, kind="ExternalInput").ap()
    ngb_d = nc.dram_tensor("ngb", [128, D], F32, kind="ExternalInput").ap()
    msk_d = nc.dram_tensor("masks", [2, 128, 8, TS], BF16, kind="ExternalInput").ap()
    out_d = nc.dram_tensor("out", [4 * TS, D], F32, kind="ExternalOutput").ap()
    kt_d = nc.dram_tensor("kt_scr", [NH, 128, S], BF16, kind="Internal").ap()
    v_d = nc.dram_tensor("v_scr", [S, 1024], BF16, kind="Internal").ap()
    wbf_d = nc.dram_tensor("wbf_scr", [48, 128, 2048], BF16, kind="Internal").ap()
    wbh_d = nc.dram_tensor("wbh_scr", [12, 128, 4096], BF16, kind="Internal").ap()

    sb = nc.alloc_sbuf_tensor
    ID = sb("ID", [128, 128], BF16)
    IDF = sb("IDF", [128, 128], F32)
    ONES = sb("ONES", [128, 128], BF16)
    NTRI = sb("NTRI", [128, 128], BF16)
    NONE = sb("NONE", [128, 128], BF16)
    PRM = sb("PRM", [128, NPRM], F32)
    DRV = sb("DRV", [128, 64], F32)
    NGB = sb("NGB", [128, D], F32)
    MSK = sb("MSK", [128, 8, TS], BF16)
    XS = [sb(f"XS{i}", [128, D], F32) for i in range(2)]
    XB0 = sb("XB0", [128, D], BF16)
    SS = sb("SS", [128, 8], F32)
    XT = [sb(f"XT{i}", [128, 16, TS], BF16) for i in range(2)]
    WBALL = sb("WBALL", [128, 4 * 2048], BF16)
    WLA = sb("WLA", [128, 8, 128], BF16)
    WLX = sb("WLX", [128, 8, 128], BF16)
    KST = [sb(f"KST{i}", [128, TS], BF16) for i in range(2)]
    VST = [sb(f"VST{i}", [128, 256], BF16) for i in range(2)]
    SQ = [sb(f"SQ{i}", [128, TS], BF16) for i in range(2)]
    RS = [sb("RS0", [128, TS], F32)]
    QT = sb("QT", [128, 8, TS], BF16)
    GSB = sb("GSB", [128, 8, TS], BF16)
    XB = [XB0[:, :], GSB[:, 4:8, :].rearrange("p a b -> p (a b)")]
    XBN = [["XB0"], ["GSB4", "GSB5", "GSB6", "GSB7"]]
    TG = [sb("TG0", [128, TS], F32)]
    XLW = [sb(f"XLW{i}", [128, TS + 3], F32) for i in range(2)]
    XC = [[sb(f"XC{q}_{i}", [128, TS], F32) for i in range(2)] for q in range(2)]
    XCB = [sb(f"XCB{i}", [128, TS], BF16) for i in range(2)]
    LT = [[sb(f"LT{i}_{j}", [128, TS], F32) for j in range(4)] for i in range(2)]
    EB0 = LT[1][3]
    EB = [EB0, EB0]
    YSEL = sb("YSEL", [128, 8, TS], BF16)
    HST = sb("HST", [128, 8], F32)
    HALO = sb("HALO", [128, 8, 4], F32)
    KTB = [sb(f"KTB{i}", [128, S], BF16) for i in range(2)]
    VB = [sb(f"VB{i}", [128, 32, 128], BF16) for i in range(2)]

    SPB = [sb(f"SPB{i}", [128, TS], BF16) for i in range(2)]
    CSB = [sb(f"CSB{i}", [128, TS], BF16) for i in range(2)]
    WWB = [sb(f"WWB{i}", [128, TS], BF16) for i in range(2)]
    YT = sb("YT", [128, 16, TS], BF16)

    PS = [nc.alloc_psum_tensor(f"PS{i}", [128, TS], F32) for i in range(8)]
    PT = [PS[6 + i][:].bitcast(BF16).rearrange("p (j c) -> p j c", c=128) for i in range(2)]

    def WB(i):
        return WBALL[:, i * 2048:(i + 1) * 2048].rearrange("p (k c) -> p k c", c=128)

    WOH = [WBALL[:, i * 4096:(i + 1) * 4096].rearrange("p (k c) -> p k c", c=256) for i in range(2)]
    WOHR = [["WB0", "WB1"], ["WB2", "WB3"]]
    WBR = ["WB0", "WB1", "WB2", "WB3"]

    P = Prog(nc)
    C_QG, C_KG, C_CW, C_CB, C_BA, C_BX, C_LAM, C_FL = 16, 17, 18, 50, 58, 66, 74, 82
    D_QGS, D_HBA, D_HBX, D_NSL, D_HNSL, D_TMP = 0, 8, 16, 24, 32, 40

    P.dma("sp", lambda e: e.dma_start(out=PRM[:], in_=prm_d), writes=["PRM"], key="PRM")
    P.dma("sp", lambda e: e.dma_start(out=NGB[:], in_=ngb_d), writes=["NGB"], key="NGB")
    P.dma("pool", lambda e: e.dma_start(out=WLA[:], in_=wla_d), writes=["WLA"], key="WLA")
    P.dma("pool", lambda e: e.dma_start(out=WLX[:], in_=wlx_d), writes=["WLX"], key="WLX")
    P.op("pool", lambda e: e.memset(IDF[:], 0.0), writes=["IDF"])
    P.op("pool", lambda e: e.affine_select(out=IDF[:], in_=IDF[:], pattern=[[-1, 128]],
                                           compare_op=ALU.not_equal, fill=1.0, base=0,
                                           channel_multiplier=1), reads=["IDF"], writes=["IDF"])
    P.op("dve", lambda e: e.tensor_copy(out=ID[:], in_=IDF[:]), reads=["IDF"], writes=["ID"])
    P.op("pool", lambda e: e.memset(ONES[:], 1.0 / 128.0), writes=["ONES"])
    P.op("pool", lambda e: e.memset(NONE[:], -1.0), writes=["NONE"])
    P.op("pool", lambda e: e.memset(IDF[:], -1.0), reads=["ID"], writes=["IDF"])
    P.op("pool", lambda e: e.affine_select(out=IDF[:], in_=IDF[:], pattern=[[-1, 128]],
                                           compare_op=ALU.is_ge, fill=0.0, base=0,
                                           channel_multiplier=1), reads=["IDF"], writes=["IDF"])
    P.op("dve", lambda e: e.tensor_copy(out=NTRI[:], in_=IDF[:]), reads=["IDF"], writes=["NTRI"])
    P.op("pool", lambda e: e.memset(HST[:], 0.0), writes=["HST"])
    P.op("pool", lambda e: e.memset(HALO[:], 0.0), writes=[f"HALO{c}" for c in range(8)])
    P.op("dve", lambda e: e.tensor_scalar(out=DRV[:, D_QGS:D_QGS + 1], in0=PRM[:, C_QG:C_QG + 1],
                                          scalar1=float(DH) ** -0.5, scalar2=None, op0=ALU.mult),
         reads=["PRM"], writes=["DRV"])
    P.op("dve", lambda e: e.tensor_scalar(out=DRV[:, D_HBA:D_HBA + 8], in0=PRM[:, C_BA:C_BA + 8],
                                          scalar1=0.5, scalar2=None, op0=ALU.mult),
         reads=["PRM"], writes=["DRV"])
    P.op("dve", lambda e: e.tensor_scalar(out=DRV[:, D_HBX:D_HBX + 8], in0=PRM[:, C_BX:C_BX + 8],
                                          scalar1=0.5, scalar2=None, op0=ALU.mult),
         reads=["PRM"], writes=["DRV"])
    P.op("act", lambda e: e.activation(out=DRV[:, D_TMP:D_TMP + 8], in_=PRM[:, C_LAM:C_LAM + 8],
                                       func=AF.Exp, scale=-1.0), reads=["PRM", "DRV"], writes=["DRVT"])
    P.op("act", lambda e: e.activation(out=DRV[:, D_TMP:D_TMP + 8], in_=DRV[:, D_TMP:D_TMP + 8],
                                       func=AF.Ln, bias=1.0), reads=["DRVT"], writes=["DRVT"])
    P.op("dve", lambda e: e.tensor_scalar(out=DRV[:, D_NSL:D_NSL + 8], in0=DRV[:, D_TMP:D_TMP + 8],
                                          scalar1=-8.0, scalar2=None, op0=ALU.mult),
         reads=["DRVT"], writes=["DRV"])
    P.op("dve", lambda e: e.tensor_scalar(out=DRV[:, D_HNSL:D_HNSL + 8], in0=DRV[:, D_TMP:D_TMP + 8],
                                          scalar1=-4.0, scalar2=None, op0=ALU.mult),
         reads=["DRVT"], writes=["DRV"])

    XSN = [[f"XS{i}_{j}" for j in range(8)] for i in range(2)]
    XCBN = [[f"XCB{i}_0", f"XCB{i}_1"] for i in range(2)]
    SQN = [[f"SQ{i}_0", f"SQ{i}_1"] for i in range(2)]
    VSLOT = [(VST[0][:, 0:256], "VST0"), (VST[1][:, 0:256], "VST1")]
    for i_ in range(2):
        for hf_ in range(2):
            VSLOT.append((XCB[i_][:, hf_ * 256:(hf_ + 1) * 256], XCBN[i_][hf_]))
            VSLOT.append((SQ[i_][:, hf_ * 256:(hf_ + 1) * 256], SQN[i_][hf_]))
    st = {"xs": 0, "vsl": 0, "xr": 0, "wb": 0, "ps": 0, "kst": 0, "vst": 0, "sq": 0, "tg": 0, "xl": 0, "xr": 0,
          "kvb": 0, "woh": 0}

    def rot(name, n):
        v = st[name]
        st[name] = (v + 1) % n
        return v

    def xt_names(xt):
        return [f"XT{xt}_{tb}_{hf}" for tb in range(4) for hf in (0, 1)]

    def norm_transpose(src, row0, xt):
        for _ in norm_transpose_gen(src, row0, xt):
            pass
        return xt_names(xt)

    def norm_transpose_gen(src, row0, xt, xb0_only=False):
        for tb in range(4):
            i = rot("xs", 2)
            xb = 0 if xb0_only else i
            r0 = row0 + tb * 128
            P.dma("sp", lambda e, i=i, r0=r0: e.dma_start(out=XS[i][:], in_=src[r0:r0 + 128, :]),
                  writes=XSN[i], key=f"XS{i}")
            P.op("act", lambda e, i=i, xb=xb: e.activation(out=XB[xb], in_=XS[i][:], func=AF.Square,
                                                    accum_out=SS[:, 2 * i:2 * i + 1]),
                 reads=XSN[i], writes=XBN[xb] + [f"SS{i}"])
            P.op("act", lambda e, i=i: e.activation(out=SS[:, 2 * i + 1:2 * i + 2], in_=SS[:, 2 * i:2 * i + 1],
                                                    func=AF.Ln, scale=1.0 / D, bias=EPS),
                 reads=[f"SS{i}"], writes=[f"SR{i}"])
            P.op("act", lambda e, i=i: e.activation(out=SS[:, 2 * i + 1:2 * i + 2],
                                                    in_=SS[:, 2 * i + 1:2 * i + 2],
                                                    func=AF.Exp, scale=-0.5),
                 reads=[f"SR{i}"], writes=[f"SR{i}"])
            P.op("dve", lambda e, i=i, xb=xb: e.scalar_tensor_tensor(out=XB[xb], in0=XS[i][:],
                                                              scalar=SS[:, 2 * i + 1:2 * i + 2],
                                                              in1=NGB[:], op0=ALU.mult, op1=ALU.mult),
                 reads=XSN[i] + [f"SR{i}", "NGB"], writes=XBN[xb])
            for hf in range(2):
                for j in range(8):
                    dc = hf * 8 + j
                    P.op("pe", lambda e, xb=xb, hf=hf, j=j, dc=dc: e.transpose(
                        out=PT[hf][:, j, :], in_=XB[xb][:, dc * 128:(dc + 1) * 128], identity=ID[:]),
                         reads=XBN[xb] + ["ID"], writes=[f"PS{6 + hf}"])
            P.op("act", lambda e, tb=tb, xt=xt: e.activation(
                out=XT[xt][:, 0:8, tb * 128:(tb + 1) * 128], in_=PT[0], func=AF.Copy),
                 reads=["PS6"], writes=[f"XT{xt}_{tb}_0"])
            P.op("dve", lambda e, tb=tb, xt=xt: e.tensor_copy(
                out=XT[xt][:, 8:16, tb * 128:(tb + 1) * 128], in_=PT[1]),
                 reads=["PS7"], writes=[f"XT{xt}_{tb}_1"])
            yield tb

    cur = {"s": 0}

    def load_wchunk(wsrc, col0):
        i = rot("wb", 4)
        cid = col0 // 128
        flat = WBALL[:, i * 2048:(i + 1) * 2048]
        if cur["s"] == 0:
            P.dma("pool", lambda e, i=i, col0=col0: e.dma_start(
                out=WB(i), in_=wsrc[:, col0:col0 + 128].rearrange("(k p) c -> p k c", p=128)),
                  writes=[WBR[i]], key=WBR[i])
            P.dma("sp", lambda e, flat=flat, cid=cid: e.dma_start(out=wbf_d[cid], in_=flat),
                  reads=[WBR[i]], writes=[f"WBF{cid}"], key=f"WBS{i}")
        else:
            P.dma("pool", lambda e, flat=flat, cid=cid: e.dma_start(out=flat, in_=wbf_d[cid]),
                  reads=[f"WBF{cid}"], writes=[WBR[i]], key=WBR[i])
        return i

    def load_woh(wsrc, col0, hid):
        i = rot("woh", 2)
        flat = WBALL[:, i * 4096:(i + 1) * 4096]
        if cur["s"] == 0:
            P.dma("pool", lambda e, col0=col0, i=i: e.dma_start(
                out=WOH[i], in_=wsrc[:, col0:col0 + 256].rearrange("(k p) c -> p k c", p=128)),
                  writes=WOHR[i], key=f"WOH{i}")
            P.dma("sp", lambda e, flat=flat, hid=hid: e.dma_start(out=wbh_d[hid], in_=flat),
                  reads=WOHR[i], writes=[f"WBH{hid}"], key=f"WHS{i}")
        else:
            P.dma("pool", lambda e, flat=flat, hid=hid: e.dma_start(out=flat, in_=wbh_d[hid]),
                  reads=[f"WBH{hid}"], writes=WOHR[i], key=f"WOH{i}")
        return i

    def proj_fm(wi, xt, xtn, b=None):
        if b is None:
            b = rot("ps", 4)
        for dc in range(16):
            P.op("pe", lambda e, b=b, wi=wi, xt=xt, dc=dc: e.matmul(
                PS[b][:], lhsT=WB(wi)[:, dc, :], rhs=XT[xt][:, dc, :], start=(dc == 0), stop=(dc == 15)),
                 reads=[WBR[wi]] + xtn, writes=[f"PS{b}"])
        return b

    def qk_norm(b, gain_ap, gain_res, dst_fn, dst_res):
        i = 0
        P.op("act", lambda e, b=b, i=i: e.activation(out=SQ[i][:], in_=PS[b][:], func=AF.Square),
             reads=[f"PS{b}"], writes=SQN[i])
        sbank = 4
        P.op("pe", lambda e, i=i, sbank=sbank: e.matmul(PS[sbank][:], lhsT=ONES[:], rhs=SQ[i][:],
                                                        start=True, stop=True),
             reads=["ONES"] + SQN[i], writes=[f"PS{sbank}"])
        P.op("act", lambda e, i=i, sbank=sbank: e.activation(out=RS[i][:], in_=PS[sbank][:], func=AF.Ln,
                                                             bias=EPS),
             reads=[f"PS{sbank}"], writes=[f"RS{i}"])
        P.op("act", lambda e, i=i: e.activation(out=RS[i][:], in_=RS[i][:], func=AF.Exp, scale=-0.5),
             reads=[f"RS{i}"], writes=[f"RS{i}"])
        P.op("dve", lambda e, b=b, i=i: e.scalar_tensor_tensor(
            out=dst_fn(), in0=PS[b][:], scalar=gain_ap, in1=RS[i][:], op0=ALU.mult, op1=ALU.mult),
             reads=[f"PS{b}", f"RS{i}", gain_res], writes=[dst_res])

    def k_norm_sq(b, ti):
        P.op("act", lambda e, b=b, ti=ti: e.activation(out=SQ[ti][:], in_=PS[b][:], func=AF.Square),
             reads=[f"PS{b}"], writes=SQN[ti])

    def k_norm_rest(b, ti, dst, dst_res):
        P.op("pe", lambda e, ti=ti: e.matmul(PS[4][:], lhsT=ONES[:], rhs=SQ[ti][:], start=True, stop=True),
             reads=["ONES"] + SQN[ti], writes=["PS4"])
        P.op("act", lambda e: e.activation(out=RS[0][:], in_=PS[4][:], func=AF.Ln, bias=EPS),
             reads=["PS4"], writes=["RS0"])
        P.op("act", lambda e: e.activation(out=RS[0][:], in_=RS[0][:], func=AF.Exp, scale=-0.5),
             reads=["RS0"], writes=["RS0"])
        P.op("dve", lambda e, b=b, dst=dst: e.scalar_tensor_tensor(
            out=dst[:], in0=PS[b][:], scalar=PRM[:, C_KG:C_KG + 1], in1=RS[0][:], op0=ALU.mult, op1=ALU.mult),
             reads=[f"PS{b}", "RS0", "PRM"], writes=[dst_res])

    def lru_front(c, banks, par):
        cw = lambda k: PRM[:, C_CW + 4 * c + k:C_CW + 4 * c + k + 1]
        cb = PRM[:, C_CB + c:C_CB + c + 1]
        for ti, b in enumerate(banks):
            xc = XC[par][ti]
            xcn = f"XC{par}_{ti}"
            P.op("dve", lambda e, b=b, ti=ti: e.tensor_copy(out=XLW[ti][:, 3:TS + 3], in_=PS[b][:]),
                 reads=[f"PS{b}"], writes=[f"XLW{ti}"])
            P.op("dve", lambda e, ti=ti: e.tensor_copy(out=XLW[ti][:, 0:3], in_=HALO[:, c, 0:3]),
                 reads=[f"HALO{c}"], writes=[f"XLWh{ti}"])
            P.op("dve", lambda e, ti=ti: e.tensor_copy(out=HALO[:, c, 0:3], in_=XLW[ti][:, TS:TS + 3]),
                 reads=[f"XLW{ti}", f"XLWh{ti}"], writes=[f"HALO{c}"])
            P.op("dve", lambda e, ti=ti, xc=xc: e.tensor_scalar(
                out=xc[:], in0=XLW[ti][:, 3:TS + 3], scalar1=cw(3), scalar2=cb, op0=ALU.mult, op1=ALU.add),
                 reads=[f"XLW{ti}", "PRM"], writes=[xcn])
            for k in (0, 1, 2):
                P.op("dve", lambda e, ti=ti, k=k, xc=xc: e.scalar_tensor_tensor(
                    out=xc[:], in0=XLW[ti][:, k:TS + k], scalar=cw(k), in1=xc[:],
                    op0=ALU.mult, op1=ALU.add),
                     reads=[f"XLW{ti}", f"XLWh{ti}", xcn, "PRM"], writes=[xcn])
            P.op("act", lambda e, ti=ti, xc=xc: e.activation(out=XCB[ti][:], in_=xc[:], func=AF.Copy),
                 reads=[xcn], writes=XCBN[ti])

    def lru_mid_gates(c):
        dv = lambda col: DRV[:, col + c:col + c + 1]
        for ti in range(2):
            L = LT[ti]
            LR = [f"LT{ti}_{j}" for j in range(4)]
            P.op("pe", lambda e, ti=ti: e.matmul(PS[5][:], lhsT=WLA[:, c, :], rhs=XCB[ti][:],
                                                 start=True, stop=True),
                 reads=["WLA"] + XCBN[ti], writes=["PS5"])
            P.op("act", lambda e, L=L: e.activation(out=L[0][:], in_=PS[5][:], func=AF.Tanh, scale=0.5,
                                                    bias=dv(D_HBA)),
                 reads=["PS5", "DRV"], writes=[LR[0]])
            P.op("pe", lambda e, ti=ti: e.matmul(PS[4][:], lhsT=WLX[:, c, :], rhs=XCB[ti][:],
                                                 start=True, stop=True),
                 reads=["WLX"] + XCBN[ti], writes=["PS4"])
            P.op("act", lambda e, L=L: e.activation(out=L[1][:], in_=PS[4][:], func=AF.Tanh, scale=0.5,
                                                    bias=dv(D_HBX)),
                 reads=["PS4", "DRV"], writes=[LR[1]])

    def lru_mid_rest(c):
        dv = lambda col: DRV[:, col + c:col + c + 1]
        for ti in range(2):
            L = LT[ti]
            LR = [f"LT{ti}_{j}" for j in range(4)]
            P.op("act", lambda e, L=L: e.activation(out=L[2][:], in_=L[0][:], func=AF.Exp, scale=dv(D_HNSL),
                                                    bias=dv(D_HNSL)),
                 reads=[LR[0], "DRV"], writes=[LR[2]])
            P.op("act", lambda e, L=L: e.activation(out=L[3][:], in_=L[0][:], func=AF.Exp, scale=dv(D_NSL),
                                                    bias=dv(D_NSL)),
                 reads=[LR[0], "DRV"], writes=[LR[3]])
        for ti in range(2):
            L = LT[ti]
            LR = [f"LT{ti}_{j}" for j in range(4)]
            P.op("act", lambda e, L=L: e.activation(out=L[3][:], in_=L[3][:], func=AF.Ln, scale=-0.25, bias=0.25),
                 reads=[LR[3]], writes=[LR[3]])
        for ti in range(2):
            L = LT[ti]
            LR = [f"LT{ti}_{j}" for j in range(4)]
            P.op("act", lambda e, L=L: e.activation(out=L[3][:], in_=L[3][:], func=AF.Exp, scale=0.5),
                 reads=[LR[3]], writes=[LR[3]])

    def lru_tail(c, par, fl0, fl1):
        for ti in range(2):
            L = LT[ti]
            LR = [f"LT{ti}_{j}" for j in range(4)]
            xc = XC[par][ti]
            xcn = f"XC{par}_{ti}"
            P.op("dve", lambda e, L=L, xc=xc: e.scalar_tensor_tensor(out=L[1][:], in0=L[1][:], scalar=1.0,
                                                                     in1=xc[:], op0=ALU.add, op1=ALU.mult),
                 reads=[LR[1], xcn], writes=[LR[1]])
            P.op("dve", lambda e, L=L: e.tensor_tensor(out=L[1][:], in0=L[1][:], in1=L[3][:], op=ALU.mult),
                 reads=[LR[1], LR[3]], writes=[LR[1]])
            if ti == 0:
                P.op("dve", lambda e, L=L: e.tensor_tensor_scan(
                    out=L[0][:], data0=L[2][:], data1=L[1][:], initial=HST[:, c:c + 1],
                    op0=ALU.mult, op1=ALU.add),
                     reads=[LR[2], LR[1], f"HST{c}", "HST"], writes=[LR[0]])
                P.op("dve", lambda e, L=L: e.tensor_scalar(
                    out=YSEL[:, c, :], in0=L[0][:], scalar1=fl0, scalar2=None, op0=ALU.mult),
                     reads=[LR[0], "PRM"], writes=[f"YSEL{c}"])
            else:
                P.op("dve", lambda e, L=L: e.tensor_tensor_scan(
                    out=L[0][:], data0=L[2][:], data1=L[1][:], initial=LT[0][0][:, TS - 1:TS],
                    op0=ALU.mult, op1=ALU.add),
                     reads=[LR[2], LR[1], "LT0_0"], writes=[LR[0]])
                P.op("dve", lambda e, L=L: e.scalar_tensor_tensor(
                    out=YSEL[:, c, :], in0=L[0][:], scalar=fl1, in1=YSEL[:, c, :],
                    op0=ALU.mult, op1=ALU.add),
                     reads=[LR[0], "PRM", f"YSEL{c}"], writes=[f"YSEL{c}"])
                P.op("dve", lambda e, L=L: e.tensor_copy(out=HST[:, c:c + 1], in_=L[0][:, TS - 1:TS]),
                     reads=[LR[0]], writes=[f"HST{c}"])

    out_res = []
    for s in range(n_slots):
        cur["s"] = s
        A, Bt = 2 * s, 2 * s + 1
        fl0 = PRM[:, C_FL + 2 * s:C_FL + 2 * s + 1]
        fl1 = PRM[:, C_FL + 2 * s + 1:C_FL + 2 * s + 2]
        P.dma("sp", lambda e, s=s: e.dma_start(out=MSK[:], in_=msk_d[s % 2]), writes=["MSK"], key="MSK")
        if s == 0:
            xtnA = norm_transpose(xall, A * TS, 0)
            xtnB = norm_transpose(xall, Bt * TS, 1)
        else:
            xtnA, xtnB = xt_names(0), xt_names(1)
        tiles = ((A, 0, xtnA), (Bt, 1, xtnB))
        def lru_proj(c):
            wi = load_wchunk(w_in, 4096 + c * 128)
            return [proj_fm(wi, xt, xtn, b=ti) for ti, (T, xt, xtn) in enumerate(tiles)]

        banks = lru_proj(0)
        lru_front(0, banks, 0)
        pend = None

        def k_finish(pend):
            cp, kb_ = pend
            for ti, (T, xt, xtn) in enumerate(tiles):
                k = rot("kst", 2)
                k_norm_rest(kb_[ti], ti, KST[k], f"KST{k}")
                P.dma("sp", lambda e, k=k, h=cp, T=T: e.dma_start(out=kt_d[h, :, T * TS:(T + 1) * TS],
                                                                  in_=KST[k][:]),
                      reads=[f"KST{k}"], writes=[f"KTd{cp}_{T}"], key=f"KST{k}")

        for c in range(8):
            lru_mid_gates(c)
            if pend is not None:
                k_finish(pend)
            lru_mid_rest(c)
            if c + 1 < 8:
                banks = lru_proj(c + 1)
            wi = load_wchunk(w_in, 1024 + c * 128)
            kbanks = []
            for ti, (T, xt, xtn) in enumerate(tiles):
                b = proj_fm(wi, xt, xtn, b=(2 if c % 2 == 0 else 6) + ti)
                k_norm_sq(b, ti)
                kbanks.append(b)
            pend = (c, kbanks)
            if c + 1 < 8:
                lru_front(c + 1, banks, (c + 1) % 2)
            lru_tail(c, c % 2, fl0, fl1)
        k_finish(pend)
        for grp in range(4):
            wh = load_woh(w_in, 2048 + grp * 256, grp)
            for (T, xt, xtn) in tiles:
                for tb in range(4):
                    b = rot("ps", 4)
                    for dc in range(16):
                        P.op("pe", lambda e, b=b, xt=xt, dc=dc, tb=tb, wh=wh: e.matmul(
                            PS[b][:, 0:256], lhsT=XT[xt][:, dc, tb * 128:(tb + 1) * 128], rhs=WOH[wh][:, dc, :],
                            start=(dc == 0), stop=(dc == 15)),
                             reads=WOHR[wh] + xtn, writes=[f"PS{b}"])
                    vi = rot("vsl", len(VSLOT))
                    vap, vnm = VSLOT[vi]
                    P.op("act", lambda e, b=b, vap=vap: e.activation(out=vap, in_=PS[b][:, 0:256], func=AF.Copy),
                         reads=[f"PS{b}"], writes=[vnm])
                    r0 = T * TS + tb * 128
                    P.dma("sp", lambda e, vap=vap, r0=r0, grp=grp: e.dma_start(
                        out=v_d[r0:r0 + 128, grp * 256:(grp + 1) * 256], in_=vap),
                          reads=[vnm], writes=[f"Vd{grp}_{T}_{tb}"], key=f"VS_{vnm}")
        xtnO = norm_transpose(xown, s * TS, 0)
        for h in range(NH):
            wi = load_wchunk(w_in, h * 128)
            b = proj_fm(wi, 0, xtnO)
            qk_norm(b, DRV[:, D_QGS:D_QGS + 1], "DRV", lambda h=h: QT[:, h, :], f"QT{h}")
        for h in range(NH):
            wi = load_wchunk(w_in, 3072 + h * 128)
            b = proj_fm(wi, 0, xtnO)
            g = 0
            P.op("act", lambda e, b=b, g=g: e.activation(out=TG[g][:], in_=PS[b][:], func=AF.Tanh, scale=0.5),
                 reads=[f"PS{b}"], writes=[f"TG{g}"])
            P.op("dve", lambda e, b=b, g=g, h=h: e.scalar_tensor_tensor(
                out=GSB[:, h, :], in0=TG[g][:], scalar=1.0, in1=PS[b][:], op0=ALU.add, op1=ALU.mult),
                 reads=[f"TG{g}", f"PS{b}"], writes=[f"GSB{h}"])
        for c in range(8):
            wi = load_wchunk(w_in, 5120 + c * 128)
            b = proj_fm(wi, 0, xtnO)
            g = 0
            P.op("act", lambda e, b=b, g=g: e.activation(out=TG[g][:], in_=PS[b][:], func=AF.Tanh, scale=0.5),
                 reads=[f"PS{b}"], writes=[f"TG{g}"])
            P.op("dve", lambda e, b=b, g=g: e.scalar_tensor_tensor(
                out=TG[g][:], in0=TG[g][:], scalar=1.0, in1=PS[b][:], op0=ALU.add, op1=ALU.mult),
                 reads=[f"TG{g}", f"PS{b}"], writes=[f"TG{g}"])
            P.op("dve", lambda e, g=g, c=c: e.scalar_tensor_tensor(
                out=YT[:, 8 + c, :], in0=TG[g][:], scalar=0.5, in1=YSEL[:, c, :], op0=ALU.mult, op1=ALU.mult),
                 reads=[f"TG{g}", f"YSEL{c}"], writes=[f"YT{8 + c}"])
        nkb = 4 * (2 * s + 2)
        ext = nkb * 128
        pre = None
        if s + 1 < n_slots:
            def _pre(s=s):
                yield from norm_transpose_gen(xall, (2 * s + 2) * TS, 0, xb0_only=True)
                yield from norm_transpose_gen(xall, (2 * s + 3) * TS, 1, xb0_only=True)
            pre = _pre()
        for h in range(NH):
            kv = rot("kvb", 2)
            P.dma("sp", lambda e, kv=kv, h=h, ext=ext: e.dma_start(out=KTB[kv][:, 0:ext], in_=kt_d[h, :, 0:ext]),
                  reads=[f"KTd{h}_{T}" for T in range(2 * s + 2)], writes=[f"KTB{kv}"], key=f"KTB{kv}")
            for part in range(nkb // 8):
                P.dma("sp", lambda e, kv=kv, h=h, part=part: e.dma_start(
                    out=VB[kv][:, part * 8:(part + 1) * 8, :],
                    in_=v_d[part * 1024:(part + 1) * 1024, h * 128:(h + 1) * 128].rearrange(
                        "(k p) d -> p k d", p=128)),
                      reads=[f"Vd{h // 2}_{T}_{tb}" for T in (2 * part, 2 * part + 1) for tb in range(4)],
                      writes=[f"VB{kv}_{part}"], key=f"VB{kv}_{part}")
            ob = 4 + (h % 2)
            units = list(range(nkb - 1, -1, -1))
            n = len(units)

            def mm_z(u, kv=kv, h=h, nkb=nkb, units=units):
                kb = units[u]
                z = u % 2
                P.op("pe", lambda e, kb=kb, z=z, kv=kv, h=h: e.matmul(
                    PS[z][:], lhsT=KTB[kv][:, kb * 128:(kb + 1) * 128], rhs=QT[:, h, :], start=True, stop=True),
                     reads=[f"KTB{kv}", f"QT{h}"], writes=[f"PS{z}"])
                P.op("act", lambda e, z=z: e.activation(out=EB[z][:], in_=PS[z][:], func=AF.Exp),
                     reads=[f"PS{z}"], writes=["LT1_3"])
                P.op("act", lambda e, z=z: e.activation(out=SPB[z][:], in_=EB[z][:], func=AF.Ln, bias=1.0),
                     reads=["LT1_3"], writes=[f"SPB{z}"])
                j = kb - (nkb - 8)
                if j >= 0:
                    P.op("dve", lambda e, z=z, j=j: e.tensor_tensor(out=SPB[z][:], in0=SPB[z][:],
                                                                    in1=MSK[:, j, :], op=ALU.mult),
                         reads=[f"SPB{z}", "MSK"], writes=[f"SPB{z}"])

            def mm_b(u, kv=kv, h=h, nkb=nkb, units=units, n=n):
                kb = units[u]
                z = u % 2
                bb = 2 + z
                last = (u == 0)
                P.op("pe", lambda e, kb=kb, bb=bb, kv=kv, h=h: e.matmul(
                    PS[bb][:], lhsT=KTB[kv][:, kb * 128:(kb + 1) * 128], rhs=QT[:, h, :], start=True, stop=False),
                     reads=[f"KTB{kv}", f"QT{h}"], writes=[f"PS{bb}"])
                P.op("pe", lambda e, z=z, bb=bb, last=last: e.matmul(PS[bb][:], lhsT=NTRI[:], rhs=SPB[z][:],
                                                                     start=False, stop=last),
                     reads=["NTRI", f"SPB{z}"], writes=[f"PS{bb}"])
                if not last:
                    P.op("pe", lambda e, z=z, bb=bb: e.matmul(PS[bb][:], lhsT=NONE[:], rhs=CSB[z][:],
                                                              start=False, stop=True),
                         reads=["NONE", f"CSB{z}"], writes=[f"PS{bb}"])
                P.op("act", lambda e, z=z, bb=bb: e.activation(out=WWB[z][:], in_=PS[bb][:], func=AF.Exp),
                     reads=[f"PS{bb}"], writes=[f"WWB{z}"])
                j = kb - (nkb - 8)
                if j >= 0:
                    P.op("dve", lambda e, z=z, j=j: e.tensor_tensor(out=WWB[z][:], in0=WWB[z][:],
                                                                    in1=MSK[:, j, :], op=ALU.mult),
                         reads=[f"WWB{z}", "MSK"], writes=[f"WWB{z}"])
                if u + 1 < n:
                    if u == 0:
                        P.op("dve", lambda e, z=z: e.tensor_copy(out=CSB[1 - z][:], in_=SPB[z][:]),
                             reads=[f"SPB{z}"], writes=[f"CSB{1 - z}"])
                    else:
                        P.op("dve", lambda e, z=z: e.tensor_tensor(out=CSB[1 - z][:], in0=CSB[z][:],
                                                                   in1=SPB[z][:], op=ALU.add),
                             reads=[f"SPB{z}", f"CSB{z}"], writes=[f"CSB{1 - z}"])

            def mm_o(u, kv=kv, ob=ob, units=units, n=n):
                kb = units[u]
                z = u % 2
                P.op("pe", lambda e, kb=kb, z=z, u=u, kv=kv, ob=ob, n=n: e.matmul(
                    PS[ob][:], lhsT=VB[kv][:, kb, :], rhs=WWB[z][:], start=(u == 0), stop=(u == n - 1)),
                     reads=[f"VB{kv}_{kb // 8}", f"WWB{z}"], writes=[f"PS{ob}"])

            for i in range(-1, n + 1):
                if 0 <= i + 1 < n:
                    mm_z(i + 1)
                if 0 <= i < n:
                    mm_b(i)
                if 0 <= i - 1 < n:
                    mm_o(i - 1)
            P.op("dve", lambda e, h=h, ob=ob: e.scalar_tensor_tensor(
                out=YT[:, h, :], in0=PS[ob][:], scalar=0.5, in1=GSB[:, h, :], op0=ALU.mult, op1=ALU.mult),
                 reads=[f"PS{ob}", f"GSB{h}"], writes=[f"YT{h}"])
            if pre is not None:
                next(pre, None)
        if pre is not None:
            for _ in pre:
                pass
        ytn = [f"YT{k}" for k in range(16)]
        items = [(grp, tb) for grp in range(8) for tb in range(4)]

        def xr_load(n):
            grp, tb = items[n]
            j = n % 8
            r0 = s * TS + tb * 128
            P.dma("pool", lambda e, j=j, r0=r0, grp=grp: e.dma_start(
                out=XS[0][:, j * 256:(j + 1) * 256], in_=xown[r0:r0 + 128, grp * 256:(grp + 1) * 256]),
                  writes=[XSN[0][j]], key=f"XR{j}")

        for n in range(8):
            xr_load(n)
        wh = None
        wh_next = load_woh(w_out, 0, 4)
        for n, (grp, tb) in enumerate(items):
            if tb == 0:
                wh = wh_next
                if grp + 1 < 8:
                    wh_next = load_woh(w_out, (grp + 1) * 256, 4 + grp + 1)
            b = rot("ps", 4)
            for ec in range(16):
                P.op("pe", lambda e, b=b, ec=ec, tb=tb, wh=wh: e.matmul(
                    PS[b][:, 0:256], lhsT=YT[:, ec, tb * 128:(tb + 1) * 128], rhs=WOH[wh][:, ec, :],
                    start=(ec == 0), stop=(ec == 15)),
                     reads=WOHR[wh] + ytn, writes=[f"PS{b}"])
            j = n % 8
            xin = XS[0][:, j * 256:(j + 1) * 256]
            xout = XS[1][:, j * 256:(j + 1) * 256]
            P.op("dve", lambda e, b=b, xin=xin, xout=xout: e.tensor_tensor(out=xout, in0=PS[b][:, 0:256], in1=xin,
                                                                         op=ALU.add),
                 reads=[f"PS{b}", XSN[0][j]], writes=[XSN[1][j]])
            nm = f"out_{s}_{grp}_{tb}"
            r0 = s * TS + tb * 128
            P.dma("sp", lambda e, xout=xout, r0=r0, grp=grp: e.dma_start(
                out=out_d[r0:r0 + 128, grp * 256:(grp + 1) * 256], in_=xout),
                  reads=[XSN[1][j]], writes=[nm], key=f"OS{j}")
            out_res.append(nm)
            if n + 8 < len(items):
                xr_load(n + 8)
    P.fence("sp", out_res)
    P.analyze().emit()
    return nc


def _own_tiles(p):
    return [2 * s + ((s + p) % 2) for s in range(4)]


def _masks(p):
    r = np.arange(128)[:, None]
    c = np.arange(TS)[None, :]
    m = np.zeros((2, 128, 8, TS), np.float32)
    for q in range(2):
        sel = (q + p) % 2
        for j in range(4):
            diag = ((j * 128 + r) < c).astype(np.float32)
            if sel == 0:
                m[q, :, j, :] = diag
                m[q, :, 4 + j, :] = 0.0
            else:
                m[q, :, j, :] = 1.0
                m[q, :, 4 + j, :] = diag
    return m.astype(ml_dtypes.bfloat16)


_NC_CACHE = {}


def kernel(x, norm_gain, w_in, q_norm_gain, k_norm_gain, conv_w, conv_b,
           lru_w_a, lru_b_a, lru_w_x, lru_b_x, lru_lambda, w_out):
    x = np.asarray(x, np.float32)
    f = lambda a: np.ascontiguousarray(np.asarray(a, np.float32))
    w_in0 = f(w_in[0])
    w_out0 = f(w_out[0])
    wla = f(np.transpose(np.asarray(lru_w_a[0]), (1, 0, 2)))
    wlx = f(np.transpose(np.asarray(lru_w_x[0]), (1, 0, 2)))
    ngb = f(np.broadcast_to(np.asarray(norm_gain[0])[None, :], (128, D)))
    col = lambda v: np.asarray(v, np.float32).reshape(8, 128).T
    in_maps = []
    for core in range(8):
        b, p = core // 2, core % 2
        prm = np.zeros((128, NPRM), np.float32)
        prm[:, 0:16] = np.asarray(norm_gain[0], np.float32).reshape(16, 128).T
        prm[:, 16] = np.asarray(q_norm_gain[0], np.float32)
        prm[:, 17] = np.asarray(k_norm_gain[0], np.float32)
        cw = np.asarray(conv_w[0], np.float32)
        for c in range(8):
            for k in range(4):
                prm[:, 18 + 4 * c + k] = cw[k, c * 128:(c + 1) * 128]
        prm[:, 50:58] = col(conv_b[0])
        prm[:, 58:66] = col(lru_b_a[0])
        prm[:, 66:74] = col(lru_b_x[0])
        prm[:, 74:82] = col(lru_lambda[0])
        own = _own_tiles(p)
        for s in range(4):
            sel = (s + p) % 2
            prm[:, 82 + 2 * s + 0] = 1.0 if sel == 0 else 0.0
            prm[:, 82 + 2 * s + 1] = 1.0 if sel == 1 else 0.0
        xb = x[b]
        xown = np.ascontiguousarray(
            np.concatenate([xb[t * TS:(t + 1) * TS] for t in own], axis=0))
        in_maps.append({
            "xall": np.ascontiguousarray(xb), "xown": xown, "w_in": w_in0, "w_out": w_out0,
            "wla": wla, "wlx": wlx, "prm": prm, "ngb": ngb, "masks": _masks(p),
        })
    if "nc" not in _NC_CACHE:
        _NC_CACHE["nc"] = build_program()
    nc = _NC_CACHE["nc"]
    res = run_bass_kernel_spmd(nc, in_maps, core_ids=list(range(8)))
    out = np.empty((B, S, D), np.float32)
    for core in range(8):
        b, p = core // 2, core % 2
        o = np.asarray(res.results[core]["out"], np.float32)
        for s, t in enumerate(_own_tiles(p)):
            out[b, t * TS:(t + 1) * TS] = o[s * TS:(s + 1) * TS]
    return out
```
